# Optimizing a Trainium2 kernel written in Bass

```python
import math
import jax, jax.numpy as jnp
from jax import lax
import numpy as np

D_MODEL = 1024
BATCH = 2
SEQ = 8192
DEPTH = 4

N_MIXERS = 2
N_CONV_LAYERS = (DEPTH + N_MIXERS - 1) // N_MIXERS
N_ATTN_LAYERS = DEPTH // N_MIXERS
CONV_WIDTH = 3
ATT_HEADS = 8
ATT_HEAD_DIM = D_MODEL // (2 * ATT_HEADS)
Q_BLOCK = 128
PEER_HEADS = 8
PEER_KEYS = 128
PEER_N_EXPERTS = PEER_KEYS * PEER_KEYS
PEER_TOPK = 16
PEER_QK_DIM = 256
PEER_HALF = PEER_QK_DIM // 2
PEER_CHUNK = 128
DEEPNORM_ALPHA = (2.0 * DEPTH) ** 0.25
DEEPNORM_BETA = (8.0 * DEPTH) ** -0.25
LN_EPS = 1e-5

kernel_name = "hybrid_shortconv_diffattn_peer_deepnorm_adaln"


def layer_norm(x, g, b):
    xf = x.astype(jnp.float32)
    mu = jnp.mean(xf, axis=-1, keepdims=True)
    var = jnp.mean(jnp.square(xf - mu), axis=-1, keepdims=True)
    return ((xf - mu) * lax.rsqrt(var + LN_EPS)).astype(x.dtype) * g + b


def rms_norm(x, g):
    xf = x.astype(jnp.float32)
    return (xf * lax.rsqrt(jnp.mean(xf * xf, axis=-1, keepdims=True) + LN_EPS)).astype(x.dtype) * g


def ada_mod(c, w, b):
    m = jax.nn.silu(c) @ w + b
    shift, scale, gate = jnp.split(m, 3, axis=-1)
    return shift[:, None, :], 1.0 + scale[:, None, :], 1.0 + gate[:, None, :]


def lambda_init_fn(layer_idx):
    return 0.8 - 0.6 * math.exp(-0.3 * layer_idx)


def short_conv_mixer(h, w_in, conv_w, w_out):
    S = h.shape[1]
    gb, gc, xv = jnp.split(h @ w_in, 3, axis=-1)
    z = gc * xv
    zp = jnp.pad(z, ((0, 0), (CONV_WIDTH - 1, 0), (0, 0)))
    zc = conv_w[0] * zp[:, 0:S] + conv_w[1] * zp[:, 1:S + 1] + conv_w[2] * zp[:, 2:S + 2]
    return (gb * zc) @ w_out


def diff_attention(h, w_qkv, lam, subln_g, w_o, lambda_init):
    B, S, _ = h.shape
    H, dh = ATT_HEADS, ATT_HEAD_DIM
    q, k, v = jnp.split(h @ w_qkv, 3, axis=-1)
    q = q.reshape(B, S, H, 2, dh) * (dh ** -0.5)
    k = k.reshape(B, S, H, 2, dh)
    v = v.reshape(B, S, H, 2 * dh)
    lam_f = lam.astype(jnp.float32)
    lam_full = (jnp.exp(jnp.sum(lam_f[0] * lam_f[1])) - jnp.exp(jnp.sum(lam_f[2] * lam_f[3]))
                + lambda_init)
    slopes = 2.0 ** (-(8.0 / H) * jnp.arange(1, H + 1, dtype=jnp.float32))
    kpos = jnp.arange(S)
    nb = S // Q_BLOCK
    qb = q.reshape(B, nb, Q_BLOCK, H, 2, dh).transpose(1, 0, 2, 3, 4, 5)

    def block(args):
        qblk, bi = args
        qpos = bi * Q_BLOCK + jnp.arange(Q_BLOCK)
        s = jnp.einsum('bqhpd,bkhpd->bhpqk', qblk, k).astype(jnp.float32)
        dist = (qpos[:, None] - kpos[None, :]).astype(jnp.float32)
        s = s - slopes[None, :, None, None, None] * dist
        s = jnp.where(dist >= 0, s, -jnp.inf)
        p = jax.nn.softmax(s, axis=-1)
        a = p[:, :, 0] - lam_full * p[:, :, 1]
        return jnp.einsum('bhqk,bkhe->bqhe', a.astype(v.dtype), v)

    o = lax.map(block, (qb, jnp.arange(nb)))
    o = o.transpose(1, 0, 2, 3, 4).reshape(B, S, H, 2 * dh)
    o = rms_norm(o, subln_g) * (1.0 - lambda_init)
    return o.reshape(B, S, H * 2 * dh) @ w_o


def peer_ffn(h, w_q, sub_keys, u, v):
    B, S, D = h.shape
    T = B * S
    hc = h.reshape(T // PEER_CHUNK, PEER_CHUNK, D)

    def chunk(xc):
        q = (xc @ w_q).reshape(PEER_CHUNK, PEER_HEADS, 2, PEER_HALF)
        s = jnp.einsum('thpd,pnd->thpn', q, sub_keys).astype(jnp.float32)
        sv, si = lax.top_k(s, PEER_TOPK)
        cand_s = (sv[:, :, 0, :, None] + sv[:, :, 1, None, :]).reshape(
            PEER_CHUNK, PEER_HEADS, PEER_TOPK * PEER_TOPK)
        cand_i = (si[:, :, 0, :, None] * PEER_KEYS + si[:, :, 1, None, :]).reshape(
            PEER_CHUNK, PEER_HEADS, PEER_TOPK * PEER_TOPK)
        top_s, pos = lax.top_k(cand_s, PEER_TOPK)
        idx = jnp.take_along_axis(cand_i, pos, axis=-1)
        g = jax.nn.softmax(top_s, axis=-1).astype(xc.dtype)
        act = jax.nn.gelu(jnp.einsum('td,thkd->thk', xc, u[idx]), approximate=False)
        return jnp.einsum('thk,thkd->td', g * act, v[idx])

    return lax.map(chunk, hc).reshape(B, S, D)


def setup_inputs(seed: int = 0) -> dict:
    key = jax.random.key(seed)
    ks = jax.random.split(key, 20)
    D = D_MODEL
    nrm = jax.random.normal
    f32 = jnp.float32
    x = nrm(ks[0], (BATCH, SEQ, D), f32)
    c = nrm(ks[1], (BATCH, D), f32)
    ada_w = nrm(ks[2], (DEPTH, 2, D, 3 * D), f32) * (0.1 * D ** -0.5)
    ada_b = nrm(ks[3], (DEPTH, 2, 3 * D), f32) * 0.02
    ln_g = 1.0 + 0.02 * nrm(ks[4], (DEPTH, 2, D), f32)
    ln_b = 0.02 * nrm(ks[5], (DEPTH, 2, D), f32)
    val_scale = jnp.concatenate([jnp.ones((2 * D,), f32), jnp.full((D,), DEEPNORM_BETA, f32)])
    conv_w_in = nrm(ks[6], (N_CONV_LAYERS, D, 3 * D), f32) * (D ** -0.5) * val_scale
    conv_w = nrm(ks[7], (N_CONV_LAYERS, CONV_WIDTH, D), f32) * (CONV_WIDTH ** -0.5)
    conv_w_out = nrm(ks[8], (N_CONV_LAYERS, D, D), f32) * (D ** -0.5) * DEEPNORM_BETA
    attn_w_qkv = nrm(ks[9], (N_ATTN_LAYERS, D, 3 * D), f32) * (D ** -0.5) * val_scale
    attn_lambda = 0.1 * nrm(ks[10], (N_ATTN_LAYERS, 4, ATT_HEAD_DIM), f32)
    attn_subln_g = 1.0 + 0.02 * nrm(ks[11], (N_ATTN_LAYERS, 2 * ATT_HEAD_DIM), f32)
    attn_w_o = nrm(ks[12], (N_ATTN_LAYERS, D, D), f32) * (D ** -0.5) * DEEPNORM_BETA
    peer_w_q = nrm(ks[13], (DEPTH, D, PEER_HEADS * PEER_QK_DIM), f32) * (D ** -0.5)
    peer_sub_keys = nrm(ks[14], (DEPTH, 2, PEER_KEYS, PEER_HALF), f32) * (PEER_HALF ** -0.5)
    peer_u = nrm(ks[15], (DEPTH, PEER_N_EXPERTS, D), f32) * (D ** -0.5) * DEEPNORM_BETA
    peer_v = nrm(ks[16], (DEPTH, PEER_N_EXPERTS, D), f32) * (PEER_HEADS ** -0.5) * DEEPNORM_BETA
    return {"x": x, "c": c, "ada_w": ada_w, "ada_b": ada_b, "ln_g": ln_g, "ln_b": ln_b,
            "conv_w_in": conv_w_in, "conv_w": conv_w, "conv_w_out": conv_w_out,
            "attn_w_qkv": attn_w_qkv, "attn_lambda": attn_lambda,
            "attn_subln_g": attn_subln_g, "attn_w_o": attn_w_o,
            "peer_w_q": peer_w_q, "peer_sub_keys": peer_sub_keys,
            "peer_u": peer_u, "peer_v": peer_v}


def reference(x, c, ada_w, ada_b, ln_g, ln_b, conv_w_in, conv_w, conv_w_out,
              attn_w_qkv, attn_lambda, attn_subln_g, attn_w_o,
              peer_w_q, peer_sub_keys, peer_u, peer_v):
    for i in range(DEPTH):
        j = i // N_MIXERS
        shift, scale, gate = ada_mod(c, ada_w[i, 0], ada_b[i, 0])
        h = x * scale + shift
        if i % N_MIXERS == 0:
            f = short_conv_mixer(h, conv_w_in[j], conv_w[j], conv_w_out[j])
        else:
            f = diff_attention(h, attn_w_qkv[j], attn_lambda[j], attn_subln_g[j],
                               attn_w_o[j], lambda_init_fn(i))
        x = layer_norm(DEEPNORM_ALPHA * x + gate * f, ln_g[i, 0], ln_b[i, 0])
        shift, scale, gate = ada_mod(c, ada_w[i, 1], ada_b[i, 1])
        h = x * scale + shift
        f = peer_ffn(h, peer_w_q[i], peer_sub_keys[i], peer_u[i], peer_v[i])
        x = layer_norm(DEEPNORM_ALPHA * x + gate * f, ln_g[i, 1], ln_b[i, 1])
    return x
```

```python
import numpy as np
from contextlib import ExitStack
import concourse.bass as bass
import concourse.mybir as mybir

F32 = mybir.dt.float32
BF16 = mybir.dt.bfloat16
U32 = mybir.dt.uint32
I32 = mybir.dt.int32
AF = mybir.ActivationFunctionType
ALU = mybir.AluOpType
AX = mybir.AxisListType

SEM_ROLL = 30000
DMA_RING = 12


class FW:
    def __init__(self, nc, es):
        self.nc = nc
        self.es = es
        self.eng = {"pe": nc.tensor, "dve": nc.vector, "act": nc.scalar,
                    "pool": nc.gpsimd, "sp": nc.sync}
        self.cur_sem = {}
        self.cnt = {}
        self.semid = 0
        for e in self.eng:
            self._new_sem(e)
        self.waited = {e: {} for e in self.eng}
        self.res = {}
        self.dma_sems = {}
        self.dma_i = {}
        self.ninst = 0

    def _alloc_sem(self, name):
        self.semid += 1
        return self.es.enter_context(self.nc.semaphore(f"{name}_{self.semid}"))

    def _new_sem(self, e):
        self.cur_sem[e] = (self._alloc_sem("s" + e), self.semid)
        self.cnt[e] = 0

    def _wait(self, e, tok):
        if tok is None:
            return
        sem, key, val = tok
        w = self.waited[e]
        if w.get(key, 0) >= val:
            return
        self.eng[e].wait_ge(sem, val)
        w[key] = val

    def _deps(self, e, reads, writes):
        toks = []
        for r in reads:
            st = self.res.get(r)
            if st and st["w"] is not None:
                toks.append(st["w"])
        for wn in writes:
            st = self.res.get(wn)
            if st:
                if st["w"] is not None:
                    toks.append(st["w"])
                toks.extend(st["r"].values())
        best = {}
        for t in toks:
            o = best.get(t[1])
            if o is None or o[2] < t[2]:
                best[t[1]] = t
        for t in best.values():
            if e == "pe" and t[1] == self.cur_sem["pe"][1]:
                continue
            self._wait(e, t)

    def _record(self, tok, reads, writes):
        for r in reads:
            st = self.res.setdefault(r, {"w": None, "r": {}})
            old = st["r"].get(tok[1])
            if old is None or old[2] < tok[2]:
                st["r"][tok[1]] = tok
        for wn in writes:
            self.res[wn] = {"w": tok, "r": {}}

    def op(self, e, fn, reads=(), writes=()):
        if self.cnt[e] >= SEM_ROLL:
            self._new_sem(e)
        self._deps(e, reads, writes)
        inst = fn(self.eng[e])
        sem, key = self.cur_sem[e]
        self.cnt[e] += 1
        inst.then_inc(sem, 1)
        tok = (sem, key, self.cnt[e])
        self._record(tok, reads, writes)
        self.ninst += 1
        return tok

    def dma(self, q, out, in_, reads=(), writes=(), **kw):
        ring = self.dma_sems.setdefault(q, [])
        i = self.dma_i.get(q, 0)
        self.dma_i[q] = i + 1
        slot = i % DMA_RING
        if len(ring) <= slot:
            ring.append([self._alloc_sem("d" + q), self.semid, 0])
        ent = ring[slot]
        sem, key, n = ent
        if n > 0:
            self._wait(q, (sem, key, 16 * n))
        self._deps(q, reads, writes)
        inst = self.eng[q].dma_start(out=out, in_=in_, **kw)
        ent[2] = n + 1
        inst.then_inc(sem, 16)
        tok = (sem, key, 16 * (n + 1))
        self._record(tok, reads, writes)
        self.ninst += 1
        return tok

    def wait_all(self, e):
        for name, st in self.res.items():
            if st["w"] is not None:
                self._wait(e, st["w"])

    def coll(self, kind, ins, outs, groups, reads=(), writes=()):
        q = "pool"
        sem = self._alloc_sem("cc")
        key = self.semid
        self._deps(q, reads, writes)
        inst = self.nc.gpsimd.collective_compute(kind, ALU.bypass, replica_groups=groups,
                                                 ins=ins, outs=outs)
        inst.then_inc(sem, 16)
        tok = (sem, key, 16)
        self._record(tok, reads, writes)
        return tok

    def barrier(self):
        toks = []
        for e in self.eng:
            sem, key = self.cur_sem[e]
            if self.cnt[e] > 0:
                toks.append((sem, key, self.cnt[e]))
        for q, ring in self.dma_sems.items():
            for sem, key, n in ring:
                if n > 0:
                    toks.append((sem, key, 16 * n))
        for name, st in self.res.items():
            if st["w"] is not None:
                toks.append(st["w"])
        for e in self.eng:
            for t in toks:
                self._wait(e, t)

from concourse.bass_utils import run_bass_kernel_spmd

D = 1024
NT = 2048
NG = NT // 256
ALPHA = 8.0 ** 0.25
EPS2 = 1e-5 / (ALPHA * ALPHA)
NEG = -1.0e30
SLOPES = [2.0 ** (-(h + 1)) for h in range(8)]

VC_ADAB = 0
VC_LNG0, VC_LNB0, VC_LNG1, VC_LNB1 = 48, 56, 64, 72
VC_CONV = 80
VC_SUBG = 80
VC_FLAG = 104
VC_LINIT = 105
NVC = 112


class K:
    pass


def emit_common(nc, es, fw, kind):
    k = K()
    k.nc, k.es, k.fw, k.kind = nc, es, fw, kind
    k.sb = lambda name, shape, dt: es.enter_context(nc.sbuf_tensor("s_" + name, shape, dt))
    k.P8 = es.enter_context(nc.psum_tensor("P8", [128, 8, 512], F32))
    sb = k.sb
    k.iota_row = sb("iota_row", [128, 128], F32)
    k.pid = sb("pid", [128, 1], F32)
    k.ident = sb("ident", [128, 128], F32)
    k.ones_f = sb("ones_f", [128, 128], F32)
    k.ones_b = sb("ones_b", [128, 128], BF16)
    k.iota16 = sb("iota16", [128, 16], F32)
    fw.op("pool", lambda e: e.iota(k.iota_row[:], [[1, 128]], base=0, channel_multiplier=0,
                                   allow_small_or_imprecise_dtypes=True), writes=["iota_row"])
    fw.op("pool", lambda e: e.iota(k.pid[:], [[0, 1]], base=0, channel_multiplier=1,
                                   allow_small_or_imprecise_dtypes=True), writes=["pid"])
    fw.op("pool", lambda e: e.iota(k.iota16[:], [[1, 16]], base=0, channel_multiplier=0,
                                   allow_small_or_imprecise_dtypes=True), writes=["iota16"])
    fw.op("dve", lambda e: e.tensor_scalar(k.ident[:], k.iota_row[:], k.pid[:, 0:1], None, op0=ALU.is_equal),
          reads=["iota_row", "pid"], writes=["ident"])
    k.iota_b = sb("iota_b", [128, 128], BF16)
    fw.op("dve", lambda e: e.tensor_copy(k.iota_b[:], k.iota_row[:]), reads=["iota_row"], writes=["iota_b"])
    fw.op("dve", lambda e: e.memset(k.ones_f[:], 1.0), writes=["ones_f"])
    fw.op("dve", lambda e: e.memset(k.ones_b[:], 1.0), writes=["ones_b"])
    return k


def emit_ada(k, d_cs, d_adaw, d_vecs):
    nc, fw, sb, P8 = k.nc, k.fw, k.sb, k.P8
    NCR = 8
    CW = 3072 // NCR
    if not hasattr(k, "vecs"):
        k.vecs = sb("vecs", [128, NVC], F32)
        k.cs = sb("cs", [128, 8], F32)
        k.silu_c = sb("silu_c", [128, 8], F32)
        k.modt = sb("modt", [128, 2, 24], F32)
        k.adawst = [sb(f"adaw{i}", [128, 8, CW], F32) for i in range(2)]
        fw.dma("sp", k.cs[:], d_cs[:, :], writes=["cs"])
        fw.op("act", lambda e: e.activation(k.silu_c[:], k.cs[:], AF.Silu), reads=["cs"], writes=["silu_c"])
    sc = k.silu_c
    wst = k.adawst
    fw.dma("sp", k.vecs[:], d_vecs[:, :], writes=["vecs"])
    i = 0
    for s in range(2):
        for cr in range(NCR):
            w = wst[i % 2]
            wn = f"adaw{i % 2}"
            fw.dma("sp", w[:], d_adaw[s].rearrange("(kc p) n -> p kc n", p=128)[:, :, cr * CW:(cr + 1) * CW],
                   writes=[wn])
            for fcl in range(CW // 128):
                fc = cr * (CW // 128) + fcl
                for kc in range(8):
                    fw.op("pe", lambda e, w=w, kc=kc, fcl=fcl, fc=fc, s=s: e.matmul(
                        P8[:, 7, s * 24 + fc:s * 24 + fc + 1], lhsT=w[:, kc, fcl * 128:(fcl + 1) * 128],
                        rhs=sc[:, kc:kc + 1], start=(kc == 0), stop=(kc == 7)),
                        reads=[wn, "silu_c"], writes=["bank7"])
            i += 1
    fw.op("dve", lambda e: e.tensor_tensor(k.modt[:].rearrange("p s n -> p (s n)"), P8[:, 7, 0:48],
                                           k.vecs[:, VC_ADAB:VC_ADAB + 48], op=ALU.add),
          reads=["bank7", "vecs"], writes=["modt"])
    fw.op("dve", lambda e: e.tensor_scalar(k.modt[:, :, 8:16], k.modt[:, :, 8:16], 1.0, None, op0=ALU.add),
          reads=["modt"], writes=["modt"])
    fw.op("dve", lambda e: e.tensor_scalar(k.modt[:, :, 16:24], k.modt[:, :, 16:24], 1.0, 1.0 / ALPHA,
                                           op0=ALU.add, op1=ALU.mult), reads=["modt"], writes=["modt"])
    k.mod = [(k.modt[:, s, 0:8], k.modt[:, s, 8:16], k.modt[:, s, 16:24]) for s in range(2)]


def emit_ln(k, y, n, gcol, bcol, out_fn, yname, scr, sqname=None):
    fw, P8 = k.fw, k.P8
    sq, mean, msq, var, rstd, tmp = scr["sq"], scr["mean"], scr["msq"], scr["var"], scr["rstd"], scr["tmp"]
    for c in range(8):
        fw.op("act", lambda e, c=c: e.activation(sq[:, c, 0:n], y[:, c, :], AF.Square), reads=[yname], writes=[sqname or f"lnsq{c}"])
    for c in range(8):
        fw.op("pe", lambda e, c=c: e.matmul(P8[:, 5, 0:n], lhsT=k.ones_f[:], rhs=y[:, c, :], start=(c == 0), stop=(c == 7)),
              reads=[yname, "ones_f"], writes=["bank5"])
    for c in range(8):
        fw.op("pe", lambda e, c=c: e.matmul(P8[:, 6, 0:n], lhsT=k.ones_f[:], rhs=sq[:, c, 0:n], start=(c == 0), stop=(c == 7)),
              reads=[sqname or f"lnsq{c}", "ones_f"], writes=["bank6"])
    fw.op("dve", lambda e: e.tensor_scalar(mean[:, 0:n], P8[:, 5, 0:n], 1.0 / D, None, op0=ALU.mult), reads=["bank5"], writes=["lnmean"])
    fw.op("dve", lambda e: e.tensor_tensor(msq[:, 0:n], mean[:, 0:n], mean[:, 0:n], op=ALU.mult), reads=["lnmean"], writes=["lnmsq"])
    fw.op("dve", lambda e: e.scalar_tensor_tensor(var[:, 0:n], P8[:, 6, 0:n], 1.0 / D, msq[:, 0:n], op0=ALU.mult, op1=ALU.subtract),
          reads=["bank6", "lnmsq"], writes=["lnvar"])
    fw.op("dve", lambda e: e.tensor_scalar(var[:, 0:n], var[:, 0:n], 0.0, EPS2, op0=ALU.max, op1=ALU.add), reads=["lnvar"], writes=["lnvar"])
    fw.op("act", lambda e: e.activation(rstd[:, 0:n], var[:, 0:n], AF.Sqrt), reads=["lnvar"], writes=["lnrstd"])
    fw.op("dve", lambda e: e.reciprocal(rstd[:, 0:n], rstd[:, 0:n]), reads=["lnrstd"], writes=["lnrstd"])
    for c in range(8):
        o, oname = out_fn(c)
        fw.op("dve", lambda e, c=c: e.tensor_tensor(tmp[:, 0:n], y[:, c, :], mean[:, 0:n], op=ALU.subtract), reads=[yname, "lnmean"], writes=["lntmp"])
        fw.op("dve", lambda e, c=c: e.tensor_tensor(tmp[:, 0:n], tmp[:, 0:n], rstd[:, 0:n], op=ALU.mult), reads=["lntmp", "lnrstd"], writes=["lntmp"])
        fw.op("dve", lambda e, c=c, o=o: e.tensor_scalar(o, tmp[:, 0:n], k.vecs[:, gcol + c:gcol + c + 1], k.vecs[:, bcol + c:bcol + c + 1],
                                                    op0=ALU.mult, op1=ALU.add), reads=["lntmp", "vecs"], writes=[oname])


def ln_scratch(k, pfx, n=512, sq=None):
    sb = k.sb
    return {"sq": sq if sq is not None else sb(pfx + "sq", [128, 8, n], F32), "mean": sb(pfx + "mean", [128, n], F32),
            "msq": sb(pfx + "msq", [128, n], F32), "var": sb(pfx + "var", [128, n], F32),
            "rstd": sb(pfx + "rstd", [128, n], F32), "tmp": sb(pfx + "tmp", [128, n], F32)}


def load_cast_w(k, dst, dname, d_w, ncols, stg, c0=0):
    fw = k.fw
    for kc in range(8):
        st = stg[kc % len(stg)]
        sn = f"{st.name}"
        fw.dma("sp", st[:, 0:ncols], d_w[kc * 128:(kc + 1) * 128, c0:c0 + ncols], writes=[sn])
        fw.op("pool", lambda e, st=st, kc=kc: e.tensor_copy(dst[:, kc, :], st[:, 0:ncols]), reads=[sn], writes=[dname])


def emit_peer(k, d_xs, d_wq, d_keysT, d_u2, d_v2, d_out, g0, ngroups, og0):
    nc, fw, sb, P8 = k.nc, k.fw, k.sb, k.P8
    sh, sc1, gt = k.mod[1]
    keysT = sb("keysT", [128, 2, 128], F32)
    fw.dma("sp", keysT[:], d_keysT.rearrange("p d n -> d p n"), writes=["keysT"])
    Wsb = sb("Wsb", [128, 256, 128], BF16)
    wq = Wsb[:, 0:128, :].rearrange("p (kc a) n -> p kc (a n)", kc=8)
    qscr, U2b, V2b = k.qscr, k.U2b, k.V2b
    xg = [sb(f"xg{i}", [128, 8, 256], F32) for i in range(2)]
    hT = [sb(f"hT{i}", [128, 8, 256], BF16) for i in range(2)]
    qT = sb("qT", [128, 16, 128], F32)
    s_sb = sb("s_sb", [128, 16, 128], F32)
    s2 = sb("s2", [128, 128], F32)
    sv = sb("sv", [128, 16, 16], F32)
    si = sb("si", [128, 16, 16], U32)
    sif = sb("sif", [128, 16, 16], F32)
    cand = sb("cand", [128, 8, 256], F32)
    cand2 = sb("cand2", [128, 256], F32)
    ts = sb("ts", [128, 8, 16], F32)
    pos = sb("pos", [128, 8, 16], U32)
    pa = sb("pa", [128, 8, 16], U32)
    pb = sb("pb", [128, 8, 16], U32)
    paf = sb("paf", [128, 8, 16], F32)
    pbf = sb("pbf", [128, 8, 16], F32)
    eq = cand[:].rearrange("p h (a b) -> p h a b", b=16)
    IJG = sb("IJG", [128, 3, 128], F32)
    ez = sb("ez", [128, 8], F32)
    IJGT = [sb(f"IJGT{i}", [128, 3, 256], F32) for i in range(2)]
    NPQ = 8
    Pb = [sb(f"Pb{i}", [128, 128], BF16) for i in range(NPQ)]
    Qb = [sb(f"Qb{i}", [128, 128], BF16) for i in range(NPQ)]
    NUV = 2
    NB = 4
    ust = [sb(f"ust{i}", [128, 1024], F32) for i in range(NUV)]
    vst = [sb(f"vst{i}", [128, 1024], F32) for i in range(NUV)]
    ub = [sb(f"ub{i}", [128, 1024], BF16) for i in range(NB)]
    vb = [sb(f"vb{i}", [128, 1024], BF16) for i in range(NB)]
    gA = [sb(f"gA{i}", [128, 256], BF16) for i in range(3)]
    WA = [sb(f"WA{i}", [128, 256], BF16) for i in range(3)]
    fsb = sb("fsb", [128, 2, 1024], F32)
    lnscr = ln_scratch(k, "p", 256, sq=fsb[:].rearrange("p t (c n) -> p (t c) n", n=256))
    sv4 = sv[:].rearrange("p (h two) k -> p h two k", two=2)
    sif4 = sif[:].rearrange("p (h two) k -> p h two k", two=2)

    def pre(g):
        b = g % 2
        x_, h_ = xg[b], hT[b]
        fw.dma("sp", x_[:], d_xs[:, :, (g0 + g) * 256:(g0 + g + 1) * 256], reads=["d_xs"], writes=[f"xg{b}"])
        for c in range(8):
            fw.op("dve", lambda e, c=c: e.tensor_scalar(h_[:, c, :], x_[:, c, :], sc1[:, c:c + 1], sh[:, c:c + 1],
                                                   op0=ALU.mult, op1=ALU.add), reads=[f"xg{b}", "modt"], writes=[f"hT{b}"])
        yield
        for tt in range(2):
            fw.dma("sp", qT[:], qscr[:, :, (g0 + g) * 256 + tt * 128:(g0 + g) * 256 + (tt + 1) * 128], reads=["qscr"], writes=["qT"])
            for q4 in range(4):
                for qq in range(4):
                    qc = q4 * 4 + qq
                    fw.op("pe", lambda e, qc=qc, qq=qq: e.matmul(P8[:, 0, qq * 128:(qq + 1) * 128],
                                                                lhsT=qT[:, qc, :], rhs=keysT[:, qc % 2, :], start=True, stop=True),
                          reads=["qT", "keysT"], writes=["bank0"])
                fw.op("act", lambda e, q4=q4: e.activation(s_sb[:, q4 * 4:q4 * 4 + 4, :], P8[:, 0, :].rearrange("p (a n) -> p a n", n=128), AF.Copy),
                      reads=["bank0"], writes=["s_sb"])
                if q4 % 2 == 1:
                    yield
            for qc in range(16):
                fw.op("dve", lambda e, qc=qc: e.max(out=sv[:, qc, 0:8], in_=s_sb[:, qc, :]), reads=["s_sb"], writes=["sv"])
                fw.op("dve", lambda e, qc=qc: e.match_replace(out=s2[:], in_to_replace=sv[:, qc, 0:8], in_values=s_sb[:, qc, :], imm_value=NEG),
                      reads=["s_sb", "sv"], writes=["s2"])
                fw.op("dve", lambda e, qc=qc: e.max(out=sv[:, qc, 8:16], in_=s2[:]), reads=["s2"], writes=["sv"])
                fw.op("dve", lambda e, qc=qc: e.max_index(out=si[:, qc, 0:8], in_max=sv[:, qc, 0:8], in_values=s_sb[:, qc, :]),
                      reads=["s_sb", "sv"], writes=["si"])
                fw.op("dve", lambda e, qc=qc: e.max_index(out=si[:, qc, 8:16], in_max=sv[:, qc, 8:16], in_values=s_sb[:, qc, :]),
                      reads=["s_sb", "sv"], writes=["si"])
                if qc % 2 == 1:
                    yield
            fw.op("dve", lambda e: e.tensor_copy(sif[:], si[:]), reads=["si"], writes=["sif"])
            fw.op("dve", lambda e: e.tensor_tensor(cand[:].rearrange("p h (a b) -> p h a b", b=16),
                                                   sv4[:, :, 0, :].unsqueeze(3).to_broadcast([128, 8, 16, 16]),
                                                   sv4[:, :, 1, :].unsqueeze(2).to_broadcast([128, 8, 16, 16]), op=ALU.add),
                  reads=["sv"], writes=["cand"])
            yield
            for hh in range(8):
                fw.op("dve", lambda e, hh=hh: e.max(out=ts[:, hh, 0:8], in_=cand[:, hh, :]), reads=["cand"], writes=["ts"])
                fw.op("dve", lambda e, hh=hh: e.match_replace(out=cand2[:], in_to_replace=ts[:, hh, 0:8], in_values=cand[:, hh, :], imm_value=NEG),
                      reads=["cand", "ts"], writes=["cand2"])
                fw.op("dve", lambda e, hh=hh: e.max(out=ts[:, hh, 8:16], in_=cand2[:]), reads=["cand2"], writes=["ts"])
                fw.op("dve", lambda e, hh=hh: e.max_index(out=pos[:, hh, 0:8], in_max=ts[:, hh, 0:8], in_values=cand[:, hh, :]),
                      reads=["cand", "ts"], writes=["pos"])
                fw.op("dve", lambda e, hh=hh: e.max_index(out=pos[:, hh, 8:16], in_max=ts[:, hh, 8:16], in_values=cand[:, hh, :]),
                      reads=["cand", "ts"], writes=["pos"])
                yield
            fw.op("dve", lambda e: e.tensor_scalar(pa[:], pos[:], 4, None, op0=ALU.logical_shift_right), reads=["pos"], writes=["pa"])
            fw.op("dve", lambda e: e.tensor_scalar(pb[:], pos[:], 15, None, op0=ALU.bitwise_and), reads=["pos"], writes=["pb"])
            fw.op("dve", lambda e: e.tensor_copy(paf[:], pa[:]), reads=["pa"], writes=["paf"])
            fw.op("dve", lambda e: e.tensor_copy(pbf[:], pb[:]), reads=["pb"], writes=["pbf"])
            yield
            for which, pf in ((0, paf), (1, pbf)):
                fw.op("dve", lambda e, pf=pf: e.tensor_tensor(eq[:], k.iota16[:].unsqueeze(1).unsqueeze(1).to_broadcast([128, 8, 16, 16]),
                                                          pf[:].unsqueeze(3).to_broadcast([128, 8, 16, 16]), op=ALU.is_equal),
                      reads=["iota16", "paf", "pbf"], writes=["cand"])
                fw.op("dve", lambda e, which=which: e.tensor_tensor(eq[:], eq[:], sif4[:, :, which, :].unsqueeze(2).to_broadcast([128, 8, 16, 16]),
                                                                op=ALU.mult), reads=["cand", "sif"], writes=["cand"])
                fw.op("dve", lambda e, which=which: e.tensor_reduce(IJG[:, which, :], eq[:].rearrange("p h k a -> p (h k) a"), axis=AX.X, op=ALU.add),
                      reads=["cand"], writes=["IJG"])
                yield
            G3 = IJG[:, 2, :].rearrange("p (h k) -> p h k", k=16)
            fw.op("dve", lambda e: e.tensor_tensor(G3, ts[:], ts[:, :, 0:1].to_broadcast([128, 8, 16]), op=ALU.subtract),
                  reads=["ts"], writes=["IJG"])
            fw.op("act", lambda e: e.activation(IJG[:, 2, :], IJG[:, 2, :], AF.Exp), reads=["IJG"], writes=["IJG"])
            fw.op("dve", lambda e: e.tensor_reduce(ez[:], G3, axis=AX.X, op=ALU.add), reads=["IJG"], writes=["ez"])
            fw.op("dve", lambda e: e.reciprocal(ez[:], ez[:]), reads=["ez"], writes=["ez"])
            fw.op("dve", lambda e: e.tensor_tensor(G3, G3, ez[:].unsqueeze(2).to_broadcast([128, 8, 16]), op=ALU.mult),
                  reads=["IJG", "ez"], writes=["IJG"])
            yield
            for w3 in range(3):
                fw.op("pe", lambda e, w3=w3: e.transpose(P8[:, 0, w3 * 128:(w3 + 1) * 128], IJG[:, w3, :], k.ident[:]),
                      reads=["IJG", "ident"], writes=["bank0"])
            fw.op("act", lambda e: e.activation(IJGT[b][:, :, tt * 128:(tt + 1) * 128],
                                                P8[:, 0, 0:384].rearrange("p (w n) -> p w n", n=128), AF.Copy),
                  reads=["bank0"], writes=[f"IJGT{b}"])
            yield
    def wbuild(g):
        b = g % 2
        for t4 in range(64):
            bk = t4 % 2
            for tq in range(4):
                t = t4 * 4 + tq
                sl = t % NPQ
                fw.op("dve", lambda e, t=t, sl=sl: e.tensor_scalar(Pb[sl][:], k.iota_b[:], IJGT[b][:, 0, t:t + 1], IJGT[b][:, 2, t:t + 1],
                                                              op0=ALU.is_equal, op1=ALU.mult), reads=["iota_b", f"IJGT{b}"], writes=[f"Pb{sl}"])
                fw.op("dve", lambda e, t=t, sl=sl: e.tensor_scalar(Qb[sl][:], k.iota_b[:], IJGT[b][:, 1, t:t + 1], None, op0=ALU.is_equal),
                      reads=["iota_b", f"IJGT{b}"], writes=[f"Qb{sl}"])
                fw.op("pe", lambda e, sl=sl, tq=tq, bk=bk: e.matmul(P8[:, bk, tq * 128:(tq + 1) * 128], lhsT=Pb[sl][:], rhs=Qb[sl][:],
                                                               start=True, stop=True), reads=[f"Pb{sl}", f"Qb{sl}"], writes=[f"bank{bk}"])
            fw.op("act", lambda e, t4=t4, bk=bk: e.activation(Wsb[:, t4 * 4:(t4 + 1) * 4, :], P8[:, bk, :].rearrange("p (a n) -> p a n", n=128), AF.Copy),
                  reads=[f"bank{bk}"], writes=["Wsb"])

    def dense(g, nxt):
        b = g % 2
        h_ = hT[b]
        def emitA(j):
            u = j % NB
            a = j % 3
            fw.dma("sp", ub[u][:], U2b[j], reads=[f"U2b{j}"], writes=[f"ub{u}"])
            fw.dma("sp", vb[u][:], V2b[j], reads=[f"V2b{j}"], writes=[f"vb{u}"])
            for kc in range(8):
                fw.op("pe", lambda e, u=u, kc=kc, a=a: e.matmul(P8[:, 3 - a, 0:256], lhsT=ub[u][:, kc * 128:(kc + 1) * 128],
                                                           rhs=h_[:, kc, :], start=(kc == 0), stop=(kc == 7)),
                      reads=[f"ub{u}", f"hT{b}"], writes=[f"bank{3 - a}"])
        emitA(0)
        emitA(1)
        for j in range(128):
            u = j % NB
            a = j % 3
            if j + 2 < 128:
                emitA(j + 2)
            fw.op("act", lambda e, a=a: e.activation(gA[a][:], P8[:, 3 - a, 0:256], AF.Gelu), reads=[f"bank{3 - a}"], writes=[f"gA{a}"])
            fw.op("pool", lambda e, a=a, j=j: e.tensor_tensor(WA[a][:], gA[a][:], Wsb[:, :, j], op=ALU.mult),
                  reads=[f"gA{a}", "Wsb"], writes=[f"WA{a}"])
            for c in range(8):
                fw.op("pe", lambda e, a=a, u=u, c=c, j=j: e.matmul(P8[:, 4 + c // 2, (c % 2) * 256:(c % 2 + 1) * 256],
                                                              lhsT=vb[u][:, c * 128:(c + 1) * 128], rhs=WA[a][:],
                                                              start=(j == 0 and c % 2 == 0), stop=(j == 127), skip_group_check=True),
                      reads=[f"WA{a}", f"vb{u}"], writes=[f"bank{4 + c // 2}"])
            if nxt is not None:
                try:
                    next(nxt)
                    next(nxt)
                except StopIteration:
                    nxt = None
        if nxt is not None:
            for _ in nxt:
                pass
        x_ = xg[b]
        for c in range(8):
            fw.op("dve", lambda e, c=c: e.scalar_tensor_tensor(x_[:, c, :], P8[:, 4 + c // 2, (c % 2) * 256:(c % 2 + 1) * 256], gt[:, c:c + 1],
                                                           x_[:, c, :], op0=ALU.mult, op1=ALU.add),
                  reads=[f"bank{4 + c // 2}", "modt", f"xg{b}"], writes=[f"xg{b}"])
        emit_ln(k, x_[:], 256, VC_LNG1, VC_LNB1, lambda c: (x_[:, c, :], f"xg{b}"), f"xg{b}", lnscr, sqname="fsb")
        fw.dma("sp", d_out[:, :, (og0 + g) * 256:(og0 + g + 1) * 256], x_[:], reads=[f"xg{b}"], writes=[k.out_name])

    for kc in range(8):
        for hf in range(2):
            st = (ust if hf == 0 else vst)[kc % NUV]
            sn = ("ust" if hf == 0 else "vst") + str(kc % NUV)
            fw.dma("sp", st[:], d_wq[kc * 128:(kc + 1) * 128, hf * 1024:(hf + 1) * 1024], writes=[sn])
            fw.op("pool", lambda e, st=st, kc=kc, hf=hf: e.tensor_copy(wq[:, kc, hf * 1024:(hf + 1) * 1024], st[:]),
                  reads=[sn], writes=["Wsb"])
    qst = [fsb[:, 0, :].rearrange("p (a n) -> p a n", n=256), fsb[:, 1, :].rearrange("p (a n) -> p a n", n=256)]
    for g in range(ngroups):
        b = g % 2
        x_, h_ = xg[b], hT[b]
        fw.dma("sp", x_[:], d_xs[:, :, (g0 + g) * 256:(g0 + g + 1) * 256], reads=["d_xs"], writes=[f"xg{b}"])
        for c in range(8):
            fw.op("dve", lambda e, c=c: e.tensor_scalar(h_[:, c, :], x_[:, c, :], sc1[:, c:c + 1], sh[:, c:c + 1],
                                                   op0=ALU.mult, op1=ALU.add), reads=[f"xg{b}", "modt"], writes=[f"hT{b}"])
        for q4 in range(4):
            for qq in range(4):
                qc = q4 * 4 + qq
                bk = qc % 2
                for kc in range(8):
                    fw.op("pe", lambda e, qc=qc, kc=kc, bk=bk: e.matmul(P8[:, bk, 0:256], lhsT=wq[:, kc, qc * 128:(qc + 1) * 128],
                                                                     rhs=h_[:, kc, :], start=(kc == 0), stop=(kc == 7)),
                          reads=["Wsb", f"hT{b}"], writes=[f"bank{bk}"])
                fw.op("act", lambda e, qq=qq, bk=bk, q4=q4: e.activation(qst[q4 % 2][:, qq, :], P8[:, bk, 0:256], AF.Copy),
                      reads=[f"bank{bk}"], writes=["fsb"])
            fw.dma("sp", qscr[:, q4 * 4:(q4 + 1) * 4, (g0 + g) * 256:(g0 + g + 1) * 256], qst[q4 % 2], reads=["fsb"], writes=["qscr"])
    for j in range(128):
        u = j % NUV
        fw.dma("sp", ust[u][:], d_u2[j], writes=[f"ust{u}"])
        fw.dma("sp", vst[u][:], d_v2[j], writes=[f"vst{u}"])
        fw.op("act", lambda e, u=u: e.activation(ub[u][:], ust[u][:], AF.Copy), reads=[f"ust{u}"], writes=[f"ub{u}"])
        fw.op("dve", lambda e, u=u: e.tensor_copy(vb[u][:], vst[u][:]), reads=[f"vst{u}"], writes=[f"vb{u}"])
        fw.dma("sp", U2b[j], ub[u][:], reads=[f"ub{u}"], writes=[f"U2b{j}"])
        fw.dma("sp", V2b[j], vb[u][:], reads=[f"vb{u}"], writes=[f"V2b{j}"])
    gen = pre(0)
    for _ in gen:
        pass
    for g in range(ngroups):
        wbuild(g)
        nxt = pre(g + 1) if g + 1 < ngroups else None
        dense(g, nxt)


class Scope:
    def __init__(self, k):
        self.k = k
    def __enter__(self):
        self.old = (self.k.es, self.k.sb)
        self.st = ExitStack()
        self.st.__enter__()
        st, nc = self.st, self.k.nc
        self.k.es = st
        def _sb(name, shape, dt):
            K.uid = getattr(K, "uid", 0) + 1
            return st.enter_context(nc.sbuf_tensor(f"s{K.uid}_" + name, shape, dt))
        self.k.sb = _sb
        return self
    def __exit__(self, *a):
        self.k.fw.barrier()
        self.st.__exit__(None, None, None)
        self.k.es, self.k.sb = self.old


def emit_conv(k, d_xin, d_win, d_wout, d_xs, nquart):
    nc, fw, sb, P8 = k.nc, k.fw, k.sb, k.P8
    sh, sc1, gt = k.mod[0]
    xq = [sb(f"xq{i}", [128, 8, 514], F32) for i in range(2)]
    hq = sb("hq", [128, 8, 514], BF16)
    zh = sb("zh", [128, 8, 514], BF16)
    gbh = sb("gbh", [128, 8, 514], BF16)
    yh = sb("yh", [128, 8, 512], BF16)
    wst = sb("wst", [128, 8, 384], F32)
    wcb = [sb(f"wcb{i}", [128, 8, 384], BF16) for i in range(2)]
    xvt = sb("xvt", [128, 512], F32)
    tt_ = sb("cvtmp", [128, 512], F32)
    wout = sb("wout", [128, 8, 1024], BF16)
    wo_st = [sb(f"wo_st{i}", [128, 1024], F32) for i in range(2)]
    lnscr = ln_scratch(k, "c", 512)
    load_cast_w(k, wout, "wout", d_wout, 1024, wo_st)
    win_v = d_win.rearrange("(kc p) n -> p kc n", p=128)
    for c in range(8):
        wb = wcb[c % 2]
        wn = f"wcb{c % 2}"
        for g3 in range(3):
            fw.dma("sp", wst[:, :, g3 * 128:(g3 + 1) * 128], win_v[:, :, g3 * 1024 + c * 128:g3 * 1024 + (c + 1) * 128], writes=["wst"])
        fw.op("pool", lambda e, wb=wb: e.tensor_copy(wb[:], wst[:]), reads=["wst"], writes=[wn])
        fw.dma("sp", k.Wc[c], wb[:], reads=[wn], writes=[f"Wc{c}"])
    wi = 0
    for q in range(nquart):
        x_ = xq[q % 2]
        xn = f"xq{q % 2}"
        c0 = q * 512
        if q == 0:
            fw.op("dve", lambda e, x_=x_: e.memset(x_[:, :, 0:2], 0.0), writes=[xn])
            fw.dma("sp", x_[:, :, 2:514], d_xin[:, :, 0:512], reads=[k.in_name], writes=[xn])
        else:
            fw.dma("sp", x_[:], d_xin[:, :, c0 - 2:c0 + 512], reads=[k.in_name], writes=[xn])
        for c in range(8):
            fw.op("dve", lambda e, c=c, x_=x_: e.tensor_scalar(hq[:, c, :], x_[:, c, :], sc1[:, c:c + 1], sh[:, c:c + 1],
                                                        op0=ALU.mult, op1=ALU.add), reads=[xn, "modt"], writes=["hq"])
        for c in range(8):
            wb = wcb[wi % 2]
            wn = f"wcb{wi % 2}"
            wi += 1
            fw.dma("sp", wb[:], k.Wc[c], reads=[f"Wc{c}"], writes=[wn])
            for si_, (t0, n) in enumerate(((0, 512), (512, 2))):
                bs = 3 * ((c * 2 + si_) % 2)
                for g3 in range(3):
                    for kc in range(8):
                        fw.op("pe", lambda e, wb=wb, g3=g3, kc=kc, bs=bs, t0=t0, n=n: e.matmul(
                            P8[:, bs + g3, 0:n], lhsT=wb[:, kc, g3 * 128:(g3 + 1) * 128], rhs=hq[:, kc, t0:t0 + n],
                            start=(kc == 0), stop=(kc == 7)), reads=[wn, "hq"], writes=[f"bank{bs + g3}"])
                fw.op("act", lambda e, bs=bs, t0=t0, n=n, c=c: e.activation(gbh[:, c, t0:t0 + n], P8[:, bs, 0:n], AF.Copy),
                      reads=[f"bank{bs}"], writes=["gbh"])
                fw.op("act", lambda e, bs=bs, n=n: e.activation(xvt[:, 0:n], P8[:, bs + 2, 0:n], AF.Copy), reads=[f"bank{bs + 2}"], writes=["xvt"])
                fw.op("dve", lambda e, bs=bs, t0=t0, n=n, c=c: e.tensor_tensor(zh[:, c, t0:t0 + n], P8[:, bs + 1, 0:n], xvt[:, 0:n], op=ALU.mult),
                      reads=[f"bank{bs + 1}", "xvt"], writes=["zh"])
        if q % 4 == 0 and q < 16:
            fc = VC_FLAG + (q // 4 if nquart == 16 else 0)
            for c in range(8):
                fw.op("dve", lambda e, c=c, fc=fc: e.tensor_scalar(zh[:, c, 0:2], zh[:, c, 0:2], k.vecs[:, fc:fc + 1], None, op0=ALU.mult),
                      reads=["zh", "vecs"], writes=["zh"])
        for c in range(8):
            w0 = k.vecs[:, VC_CONV + c:VC_CONV + c + 1]
            w1 = k.vecs[:, VC_CONV + 8 + c:VC_CONV + 8 + c + 1]
            w2 = k.vecs[:, VC_CONV + 16 + c:VC_CONV + 16 + c + 1]
            fw.op("dve", lambda e, c=c, w2=w2: e.tensor_scalar(tt_[:], zh[:, c, 2:514], w2, None, op0=ALU.mult), reads=["zh", "vecs"], writes=["cvtmp"])
            fw.op("dve", lambda e, c=c, w1=w1: e.scalar_tensor_tensor(tt_[:], zh[:, c, 1:513], w1, tt_[:], op0=ALU.mult, op1=ALU.add),
                  reads=["zh", "vecs", "cvtmp"], writes=["cvtmp"])
            fw.op("dve", lambda e, c=c, w0=w0: e.scalar_tensor_tensor(tt_[:], zh[:, c, 0:512], w0, tt_[:], op0=ALU.mult, op1=ALU.add),
                  reads=["zh", "vecs", "cvtmp"], writes=["cvtmp"])
            fw.op("dve", lambda e, c=c: e.tensor_tensor(yh[:, c, :], tt_[:], gbh[:, c, 2:514], op=ALU.mult), reads=["cvtmp", "gbh"], writes=["yh"])
        for oc in range(8):
            bk = 6 + oc % 2
            for kc in range(8):
                fw.op("pe", lambda e, oc=oc, kc=kc, bk=bk: e.matmul(P8[:, bk, :], lhsT=wout[:, kc, oc * 128:(oc + 1) * 128], rhs=yh[:, kc, :],
                                                               start=(kc == 0), stop=(kc == 7)), reads=["wout", "yh"], writes=[f"bank{bk}"])
            fw.op("dve", lambda e, oc=oc, bk=bk, x_=x_: e.scalar_tensor_tensor(x_[:, oc, 2:514], P8[:, bk, :], gt[:, oc:oc + 1],
                                                                           x_[:, oc, 2:514], op0=ALU.mult, op1=ALU.add),
                  reads=[f"bank{bk}", "modt", xn, "hq"], writes=[xn])
        y = x_[:, :, 2:514]
        emit_ln(k, y, 512, VC_LNG0, VC_LNB0, lambda c, y=y: (y[:, c, :], xn), xn, lnscr)
        fw.dma("sp", d_xs[:, :, c0:c0 + 512], y, reads=[xn], writes=["d_xs"])


def emit_attn(k, d_xin, d_wqkv, d_wo, d_kbias, d_lam, d_xs, nqb):
    nc, fw, P8 = k.nc, k.fw, k.P8
    sh, sc1, gt = k.mod[0]
    KTs, Vs, QTs, ATs = k.KTs, k.Vs, k.QTs, k.ATs
    G0 = 16 - nqb
    with Scope(k):
        sb = k.sb
        wqb = sb("wqb", [128, 8, 1024], BF16)
        wk = sb("wk", [128, 8, 1024], BF16)
        wv = sb("wv", [128, 8, 1024], BF16)
        stg = [sb(f"astg{i}", [128, 1024], F32) for i in range(2)]
        load_cast_w(k, wqb, "wqb", d_wqkv, 1024, stg, c0=0)
        load_cast_w(k, wk, "wk", d_wqkv, 1024, stg, c0=1024)
        load_cast_w(k, wv, "wv", d_wqkv, 1024, stg, c0=2048)
        xc = [sb(f"xc{i}", [128, 8, 512], F32) for i in range(2)]
        hTc = sb("hTc", [128, 8, 512], BF16)
        kst = sb("kst", [128, 8, 512], BF16)
        qst = sb("qst", [128, 8, 512], BF16)
        vstg = sb("vstg", [128, 4, 1024], BF16)
        for cb in range(16):
            x_ = xc[cb % 2]
            xn = f"xc{cb % 2}"
            fw.dma("sp", x_[:], d_xin[:, :, cb * 512:(cb + 1) * 512], reads=[k.in_name], writes=[xn])
            for c in range(8):
                fw.op("dve", lambda e, c=c, x_=x_: e.tensor_scalar(hTc[:, c, :], x_[:, c, :], sc1[:, c:c + 1], sh[:, c:c + 1],
                                                            op0=ALU.mult, op1=ALU.add), reads=[xn, "modt"], writes=["hTc"])
            for h in range(8):
                bk = h % 2
                for kc in range(8):
                    fw.op("pe", lambda e, h=h, kc=kc, bk=bk: e.matmul(P8[:, bk, :], lhsT=wk[:, kc, h * 128:(h + 1) * 128], rhs=hTc[:, kc, :],
                                                                 start=(kc == 0), stop=(kc == 7)), reads=["wk", "hTc"], writes=[f"bank{bk}"])
                fw.op("act", lambda e, h=h, bk=bk: e.activation(kst[:, h, :], P8[:, bk, :], AF.Copy), reads=[f"bank{bk}"], writes=["kst"])
            fw.dma("sp", KTs.rearrange("h p t -> p h t")[:, :, cb * 512:(cb + 1) * 512], kst[:], reads=["kst"], writes=["KTs"])
            for tt in range(4):
                for half in range(2):
                    bk = 2 + half
                    for kc in range(8):
                        fw.op("pe", lambda e, tt=tt, half=half, kc=kc, bk=bk: e.matmul(
                            P8[:, bk, :], lhsT=hTc[:, kc, tt * 128:(tt + 1) * 128], rhs=wv[:, kc, half * 512:(half + 1) * 512],
                            start=(kc == 0), stop=(kc == 7)), reads=["wv", "hTc"], writes=[f"bank{bk}"])
                    fw.op("dve", lambda e, tt=tt, half=half, bk=bk: e.tensor_copy(vstg[:, tt, half * 512:(half + 1) * 512], P8[:, bk, :]),
                          reads=[f"bank{bk}"], writes=["vstg"])
            for h in range(8):
                fw.dma("sp", Vs[h][:, cb * 4:(cb + 1) * 4, :], vstg[:, :, h * 128:(h + 1) * 128], reads=["vstg"], writes=["Vs"])
            if cb >= G0:
                for h in range(8):
                    bk = 4 + h % 2
                    for kc in range(8):
                        fw.op("pe", lambda e, h=h, kc=kc, bk=bk: e.matmul(P8[:, bk, :], lhsT=wqb[:, kc, h * 128:(h + 1) * 128], rhs=hTc[:, kc, :],
                                                                     start=(kc == 0), stop=(kc == 7)), reads=["wqb", "hTc"], writes=[f"bank{bk}"])
                    fw.op("act", lambda e, h=h, bk=bk, cb=cb: e.activation(qst[:, h, :], P8[:, bk, :], AF.Copy, scale=0.125),
                          reads=[f"bank{bk}"], writes=["qst"])
                fw.dma("sp", QTs.rearrange("h p t -> p h t")[:, :, cb * 512:(cb + 1) * 512], qst[:], reads=["qst"], writes=["QTs"])
    with Scope(k):
        sb = k.sb
        lam = sb("lam", [128, 256], F32)
        lp = sb("lp", [128, 2, 64], F32)
        l2 = sb("l2", [128, 2], F32)
        neglam = sb("neglam", [128, 1], F32)
        fw.dma("sp", lam[:], d_lam[:, :], writes=["lam"])
        lam4 = lam[:].rearrange("p (a b d) -> p a b d", a=2, b=2)
        fw.op("dve", lambda e: e.tensor_tensor(lp[:], lam4[:, :, 0, :], lam4[:, :, 1, :], op=ALU.mult), reads=["lam"], writes=["lp"])
        fw.op("dve", lambda e: e.tensor_reduce(l2[:], lp[:], axis=AX.X, op=ALU.add), reads=["lp"], writes=["l2"])
        fw.op("act", lambda e: e.activation(l2[:], l2[:], AF.Exp), reads=["l2"], writes=["l2"])
        fw.op("dve", lambda e: e.tensor_tensor(neglam[:], l2[:, 1:2], l2[:, 0:1], op=ALU.subtract), reads=["l2"], writes=["neglam"])
        fw.op("dve", lambda e: e.tensor_tensor(neglam[:], neglam[:], k.vecs[:, VC_LINIT:VC_LINIT + 1], op=ALU.subtract),
              reads=["neglam", "vecs"], writes=["neglam"])
        D0 = sb("D0", [128, 512], F32)
        fw.op("pool", lambda e: e.iota(D0[:], [[-1, 512]], base=0, channel_multiplier=1, allow_small_or_imprecise_dtypes=True), writes=["D0"])
        tri = sb("tri", [128, 128], BF16)
        fw.op("dve", lambda e: e.tensor_scalar(tri[:], D0[:, 0:128], 0.0, None, op0=ALU.is_le), reads=["D0"], writes=["tri"])
        kbias = sb("kbias", [128, 64], F32)
        fw.dma("sp", kbias[:], d_kbias[:, :], writes=["kbias"])
        cbt = sb("cbt", [128, 8, 64, 16], F32)
        fw.op("pool", lambda e: e.iota(cbt[:], [[0, 8], [1, 64], [-4, 16]], base=0, channel_multiplier=0, allow_small_or_imprecise_dtypes=True),
              writes=["cbt"])
        for h in range(8):
            fw.op("dve", lambda e, h=h: e.scalar_tensor_tensor(cbt[:, h], cbt[:, h], SLOPES[h] * 128.0,
                                                           kbias[:].unsqueeze(2).to_broadcast([128, 64, 16]), op0=ALU.mult, op1=ALU.add),
                  reads=["cbt", "kbias"], writes=["cbt"])
        Kh = sb("Kh", [128, 4 * NT], BF16)
        Vh = sb("Vh", [128, 64, 128], BF16)
        Sp = [sb(f"Sp{i}", [128, 2, 512], F32) for i in range(2)]
        Pt = [sb(f"Pt{i}", [128, 2, 512], BF16) for i in range(2)]
        rz = sb("rz", [128, 2, 512], F32)
        o0 = sb("o0", [128, 512], F32)
        o1 = sb("o1", [128, 512], F32)
        osq = sb("osq", [128, 512], F32)
        rs = sb("rs", [128, 512], F32)
        qsl = [sb(f"qsl{i}", [128, 512], BF16) for i in range(2)]
        aout = [sb(f"aout{i}", [128, 512], BF16) for i in range(2)]
        iters = []
        for h in range(8):
            for qg in range(G0, 16):
                nkb = 4 * (qg + 1)
                for kb in range(nkb):
                    r = kb - 4 * qg
                    qlo = max(0, 128 * r)
                    iters.append((h, qg, kb, nkb, r, qlo, 512 - qlo, len(iters) % 2))

        def emit_qk(i):
            h, qg, kb, nkb, r, qlo, n, bf = iters[i]
            gi = (h * nqb + qg - G0) % 2
            if qg == G0 and kb == 0:
                fw.dma("sp", Kh[:], KTs[h], reads=["KTs"], writes=["Kh"])
            if kb == 0:
                fw.dma("sp", qsl[gi][:], QTs[h][:, qg * 512:(qg + 1) * 512], reads=["QTs"], writes=[f"qsl{gi}"])
            for p in range(2):
                fw.op("pe", lambda e, p=p: e.matmul(
                    P8[:, bf * 2 + p, 0:n], lhsT=Kh[p * 64:(p + 1) * 64, kb * 128:(kb + 1) * 128],
                    rhs=qsl[gi][p * 64:(p + 1) * 64, qlo:512], start=True, stop=True),
                    reads=["Kh", f"qsl{gi}"], writes=[f"bank{bf * 2 + p}"])

        def emit_mid(i):
            h, qg, kb, nkb, r, qlo, n, bf = iters[i]
            for p in range(2):
                fw.op("dve", lambda e, p=p: e.scalar_tensor_tensor(
                    Sp[bf][:, p, 0:n], D0[:, qlo:512], SLOPES[h], P8[:, bf * 2 + p, 0:n],
                    op0=ALU.mult, op1=ALU.add), reads=["D0", f"bank{bf * 2 + p}"], writes=[f"Sp{bf}_{p}"])
                fw.op("act", lambda e, p=p: e.activation(Pt[bf][:, p, 0:n], Sp[bf][:, p, 0:n], AF.Exp, bias=cbt[:, h, kb, qg:qg + 1]),
                      reads=[f"Sp{bf}_{p}", "cbt"], writes=[f"Pt{bf}_{p}"])
                if r >= 0:
                    fw.op("pool", lambda e, p=p: e.tensor_tensor(Pt[bf][:, p, 0:128], Pt[bf][:, p, 0:128], tri[:], op=ALU.mult),
                          reads=[f"Pt{bf}_{p}", "tri"], writes=[f"Pt{bf}_{p}"])

        def emit_pv(i):
            h, qg, kb, nkb, r, qlo, n, bf = iters[i]
            if qg == G0 and kb == 0:
                fw.dma("sp", Vh[:], Vs[h], reads=["Vs"], writes=["Vh"])
            for p in range(2):
                fw.op("pe", lambda e, p=p: e.matmul(
                    P8[:, 4 + p, qlo:512], lhsT=Vh[:, kb, :], rhs=Pt[bf][:, p, 0:n], start=(kb == 0), stop=(kb == nkb - 1)),
                    reads=["Vh", f"Pt{bf}_{p}"], writes=[f"bank{4 + p}"])
                fw.op("pe", lambda e, p=p: e.matmul(
                    P8[:, 6 + p, qlo:512], lhsT=k.ones_b[:], rhs=Pt[bf][:, p, 0:n], start=(kb == 0), stop=(kb == nkb - 1)),
                    reads=["ones_b", f"Pt{bf}_{p}"], writes=[f"bank{6 + p}"])
            if kb == nkb - 1:
                emit_epi(h, qg)

        def emit_epi(h, qg):
            fw.op("dve", lambda e: e.tensor_scalar(rz[:], P8[:, 6:8, :], 1e-30, None, op0=ALU.max), reads=["bank6", "bank7"], writes=["rz"])
            fw.op("dve", lambda e: e.reciprocal(rz[:], rz[:]), reads=["rz"], writes=["rz"])
            fw.op("dve", lambda e: e.tensor_tensor(o0[:], P8[:, 4, :], rz[:, 0, :], op=ALU.mult), reads=["bank4", "rz"], writes=["o0"])
            fw.op("dve", lambda e: e.tensor_tensor(o1[:], P8[:, 5, :], rz[:, 1, :], op=ALU.mult), reads=["bank5", "rz"], writes=["o1"])
            fw.op("dve", lambda e: e.scalar_tensor_tensor(o0[:], o1[:], neglam[:, 0:1], o0[:], op0=ALU.mult, op1=ALU.add),
                  reads=["o0", "o1", "neglam"], writes=["o0"])
            fw.op("act", lambda e: e.activation(osq[:], o0[:], AF.Square), reads=["o0"], writes=["osq"])
            fw.op("pe", lambda e: e.matmul(P8[:, 6, :], lhsT=k.ones_f[:], rhs=osq[:], start=True, stop=True), reads=["ones_f", "osq", "rz"], writes=["bank6"])
            fw.op("dve", lambda e: e.tensor_scalar(rs[:], P8[:, 6, :], 1.0 / 128.0, 1e-5, op0=ALU.mult, op1=ALU.add), reads=["bank6"], writes=["rs"])
            fw.op("act", lambda e: e.activation(rs[:], rs[:], AF.Sqrt), reads=["rs"], writes=["rs"])
            fw.op("dve", lambda e: e.reciprocal(rs[:], rs[:]), reads=["rs"], writes=["rs"])
            fw.op("dve", lambda e: e.tensor_tensor(o0[:], o0[:], rs[:], op=ALU.mult), reads=["o0", "rs"], writes=["o0"])
            gi = (h * nqb + qg - G0) % 2
            fw.op("dve", lambda e: e.tensor_scalar(aout[gi][:], o0[:], k.vecs[:, VC_SUBG:VC_SUBG + 1],
                                                   k.vecs[:, VC_LINIT + 1:VC_LINIT + 2], op0=ALU.mult, op1=ALU.mult),
                  reads=["o0", "vecs"], writes=[f"aout{gi}"])
            fw.dma("sp", ATs[h][:, qg * 512:(qg + 1) * 512], aout[gi][:], reads=[f"aout{gi}"], writes=["ATs"])

        emit_qk(0)
        for i in range(len(iters)):
            if i + 1 < len(iters):
                emit_qk(i + 1)
            emit_mid(i)
            emit_pv(i)
    with Scope(k):
        sb = k.sb
        xo = [sb(f"xo{i}", [128, 8, 512], F32) for i in range(2)]
        at = [sb(f"at{i}", [128, 8, 512], BF16) for i in range(2)]
        wo = sb("wo", [128, 8, 1024], BF16)
        stg = [sb(f"ostg{i}", [128, 1024], F32) for i in range(2)]
        load_cast_w(k, wo, "wo", d_wo, 1024, stg)
        lnscr = ln_scratch(k, "a", 512)
        for G in range(G0, 16):
            i2 = G % 2
            x_, a_ = xo[i2], at[i2]
            fw.dma("sp", x_[:], d_xin[:, :, G * 512:(G + 1) * 512], reads=[k.in_name], writes=[f"xo{i2}"])
            fw.dma("sp", a_[:], ATs.rearrange("h p t -> p h t")[:, :, G * 512:(G + 1) * 512], reads=["ATs"], writes=[f"at{i2}"])
            for oc in range(8):
                bk = oc % 2
                for kc in range(8):
                    fw.op("pe", lambda e, oc=oc, kc=kc, bk=bk, a_=a_: e.matmul(P8[:, bk, :], lhsT=wo[:, kc, oc * 128:(oc + 1) * 128],
                                                                         rhs=a_[:, kc, :], start=(kc == 0), stop=(kc == 7)),
                          reads=["wo", f"at{i2}"], writes=[f"bank{bk}"])
                fw.op("dve", lambda e, oc=oc, bk=bk, x_=x_: e.scalar_tensor_tensor(x_[:, oc, :], P8[:, bk, :], gt[:, oc:oc + 1],
                                                                               x_[:, oc, :], op0=ALU.mult, op1=ALU.add),
                      reads=[f"bank{bk}", "modt", f"xo{i2}"], writes=[f"xo{i2}"])
            emit_ln(k, x_[:], 512, VC_LNG0, VC_LNB0, lambda c, x_=x_: (x_[:, c, :], f"xo{i2}"), f"xo{i2}", lnscr)
            fw.dma("sp", d_xs[:, :, G * 512:(G + 1) * 512], x_[:], reads=[f"xo{i2}"], writes=["d_xs"])


NV = 4 * NT


def build_fused():
    nc = bass.Bass("TRN2", target_bir_lowering=False)
    dt = lambda name, shape, dtype, kind: nc.dram_tensor(name, shape, dtype, kind=kind).ap()
    EI = "ExternalInput"
    d_x = dt("xin", [128, 8, NV], F32, EI)
    d_cs = dt("cs", [128, 8], F32, EI)
    d_adaw = dt("adaw", [4, 2, 1024, 3072], F32, EI)
    d_vecs = dt("vecs", [4, 128, NVC], F32, EI)
    d_w1 = dt("w1", [4, 1024, 3072], F32, EI)
    d_w2 = dt("w2", [4, 1024, 1024], F32, EI)
    d_wq = dt("wq", [4, 1024, 2048], F32, EI)
    d_keysT = dt("keysT", [4, 2, 128, 128], F32, EI)
    d_u2 = dt("u2", [4, 128, 128, 1024], F32, EI)
    d_v2 = dt("v2", [4, 128, 128, 1024], F32, EI)
    d_kbias = dt("kbias", [128, 64], F32, EI)
    d_lam = dt("lam", [2, 128, 256], F32, EI)
    d_out = dt("out", [128, 8, NT], F32, "ExternalOutput")
    IN = "Internal"
    d_xs = dt("xs", [128, 8, NV], F32, IN)
    xA = dt("xA", [128, 8, NV], F32, IN)
    xB = dt("xB", [128, 8, NV], F32, IN)
    with ExitStack() as es:
        fw = FW(nc, es)
        k = emit_common(nc, es, fw, 2)
        k.qscr = dt("qscr", [128, 16, NV], F32, IN)
        k.U2b = dt("U2b", [128, 128, 1024], BF16, IN)
        k.V2b = dt("V2b", [128, 128, 1024], BF16, IN)
        k.KTs = dt("KTs", [8, 128, NV], BF16, IN)
        k.Vs = dt("Vs", [8, 128, 64, 128], BF16, IN)
        k.QTs = dt("QTs", [8, 128, NV], BF16, IN)
        k.ATs = dt("ATs", [8, 128, NV], BF16, IN)
        k.Wc = dt("Wc", [8, 128, 8, 384], BF16, IN)
        chain = [(d_x, "xin", xA, "xA"), (xA, "xA", xB, "xB"), (xB, "xB", xA, "xA"), (xA, "xA", d_out, "out")]
        for l in range(4):
            src, k.in_name, dst, k.out_name = chain[l]
            emit_ada(k, d_cs, d_adaw[l], d_vecs[l])
            with Scope(k):
                if l % 2 == 0:
                    emit_conv(k, src, d_w1[l], d_w2[l], d_xs, 16)
                else:
                    emit_attn(k, src, d_w1[l], d_w2[l], d_kbias, d_lam[l // 2], d_xs, 16 if l == 1 else 4)
            with Scope(k):
                if l < 3:
                    emit_peer(k, d_xs, d_wq[l], d_keysT[l], d_u2[l], d_v2[l], dst, 0, 32, 0)
                else:
                    emit_peer(k, d_xs, d_wq[l], d_keysT[l], d_u2[l], d_v2[l], dst, 24, 8, 0)
        fw.barrier()
    return nc


def fm(a):
    T = a.shape[0]
    return np.ascontiguousarray(a.reshape(T, 8, 128).transpose(2, 1, 0))


def col(v):
    return np.ascontiguousarray(v.reshape(-1, 128).T)


def core_inputs(inp):
    x = np.ascontiguousarray(inp["x"], dtype=np.float32)
    u2 = np.ascontiguousarray(inp["peer_u"].reshape(4, 128, 128, 8, 128).transpose(0, 2, 4, 3, 1)).reshape(4, 128, 128, 1024)
    v2 = np.ascontiguousarray(inp["peer_v"].reshape(4, 128, 128, 1024).transpose(0, 2, 1, 3))
    keysT = np.ascontiguousarray(inp["peer_sub_keys"].transpose(0, 1, 3, 2))
    w1 = np.stack([inp["conv_w_in"][0], inp["attn_w_qkv"][0], inp["conv_w_in"][1], inp["attn_w_qkv"][1]])
    w2 = np.stack([inp["conv_w_out"][0], inp["attn_w_o"][0], inp["conv_w_out"][1], inp["attn_w_o"][1]])
    lam = np.ascontiguousarray(np.broadcast_to(inp["attn_lambda"].reshape(2, 1, 256), (2, 128, 256)))
    maps = []
    for cid in range(8):
        b, r = cid // 4, cid % 4
        vecs = np.zeros((4, 128, NVC), np.float32)
        for i in range(4):
            j = i // 2
            v = vecs[i]
            v[:, 0:24] = col(inp["ada_b"][i, 0])
            v[:, 24:48] = col(inp["ada_b"][i, 1])
            v[:, VC_LNG0:VC_LNG0 + 8] = col(inp["ln_g"][i, 0])
            v[:, VC_LNB0:VC_LNB0 + 8] = col(inp["ln_b"][i, 0])
            v[:, VC_LNG1:VC_LNG1 + 8] = col(inp["ln_g"][i, 1])
            v[:, VC_LNB1:VC_LNB1 + 8] = col(inp["ln_b"][i, 1])
            if i % 2 == 0:
                for tap in range(3):
                    v[:, VC_CONV + tap * 8:VC_CONV + tap * 8 + 8] = col(inp["conv_w"][j, tap])
                for qi in range(4):
                    v[:, VC_FLAG + qi] = 0.0 if qi == 3 - r else 1.0
            else:
                v[:, VC_SUBG] = inp["attn_subln_g"][j]
                li = 0.8 - 0.6 * float(np.exp(-0.3 * i))
                v[:, VC_LINIT] = li
                v[:, VC_LINIT + 1] = 1.0 - li
        ctx = np.zeros((NV, D), np.float32)
        ctx[(3 - r) * NT:] = x[b, 0:(r + 1) * NT]
        kb = np.zeros((64, 128), np.float32)
        kb[:(3 - r) * 16] = -30000.0
        maps.append({"xin": fm(ctx), "cs": col(inp["c"][b]), "adaw": inp["ada_w"], "vecs": vecs, "w1": w1, "w2": w2,
                     "wq": inp["peer_w_q"], "keysT": keysT, "u2": u2, "v2": v2,
                     "kbias": np.ascontiguousarray(kb.T), "lam": lam})
    return maps


_PROG = []


def kernel(**inputs):
    inp = {k_: np.asarray(v) for k_, v in inputs.items()}
    if not _PROG:
        _PROG.append(build_fused())
    res = run_bass_kernel_spmd(_PROG[0], core_inputs(inp), core_ids=list(range(8)))
    out = np.empty((2, 4 * NT, D), np.float32)
    for cid in range(8):
        b, r = cid // 4, cid % 4
        o = res.results[cid]["out"]
        out[b, r * NT:(r + 1) * NT] = o.transpose(2, 1, 0).reshape(NT, D)
    return out
```
